# Optimizing a Trainium2 kernel written in Bass

```python
import jax, jax.numpy as jnp
from jax import lax
import numpy as np

D_MODEL = 1024
BATCH = 4
SEQ = 4096
DEPTH = 4

N_META = 16
POOL_WIDTH = D_MODEL
POOL_GROUPS = 4
POOL_GROUP_DIM = POOL_WIDTH // POOL_GROUPS
POOL_WINDOWS = (2, 4, 8, 16)
SSM_EXPAND = 2
D_INNER = SSM_EXPAND * D_MODEL
SSM_HEAD_DIM = 64
SSM_HEADS = D_INNER // SSM_HEAD_DIM
SSM_GROUPS = 8
HEADS_PER_GROUP = SSM_HEADS // SSM_GROUPS
D_STATE = 128
CONV_WIDTH = 4
CHUNK = 128
D_XBC = D_INNER + 2 * SSM_GROUPS * D_STATE
D_FF = 4 * D_MODEL
N_BRANCHES = 2
EPS = 1e-5

OFF_POOL = 0
OFF_Z = OFF_POOL + POOL_WIDTH
OFF_XBC = OFF_Z + D_INNER
OFF_DT = OFF_XBC + D_XBC
OFF_GATE = OFF_DT + SSM_HEADS
IN_COLS = OFF_GATE + N_BRANCHES * D_MODEL

kernel_name = "hybrid_pool_ssd_gated_parallel"


def rmsnorm(x, w):
    xf = x.astype(jnp.float32)
    xf = xf * lax.rsqrt(jnp.mean(xf * xf, axis=-1, keepdims=True) + EPS)
    return xf.astype(x.dtype) * w


def pool_mixer(u, w_group, scale):
    bsz, seqlen, _ = u.shape
    ug = u.reshape(bsz, seqlen, POOL_GROUPS, POOL_GROUP_DIM)
    pos = jnp.arange(seqlen)[None, :, None]
    outs = []
    for g, win in enumerate(POOL_WINDOWS):
        xg = ug[:, :, g, :]
        cs = jnp.cumsum(xg.astype(jnp.float32), axis=1)
        shifted = jnp.pad(cs, ((0, 0), (win, 0), (0, 0)))[:, :seqlen]
        count = jnp.minimum(pos + 1, win).astype(jnp.float32)
        mean = (cs - shifted) / count
        outs.append(mean.astype(u.dtype) - xg)
    pooled = jnp.stack(outs, axis=2)
    y = jnp.einsum("blgc,gcd->blgd", pooled, w_group).reshape(bsz, seqlen, POOL_WIDTH)
    return y * scale


def causal_depthwise_conv(x, w, b):
    seqlen = x.shape[1]
    xp = jnp.pad(x, ((0, 0), (CONV_WIDTH - 1, 0), (0, 0)))
    y = b
    for k in range(CONV_WIDTH):
        y = y + xp[:, k:k + seqlen] * w[k]
    return y


def ssd_chunked(x, dt, a, b_mat, c_mat):
    bsz, seqlen = x.shape[0], x.shape[1]
    pad = (-N_META) % CHUNK
    lp = seqlen + pad
    nc = lp // CHUNK

    def fpad(t):
        return jnp.pad(t, [(0, 0), (pad, 0)] + [(0, 0)] * (t.ndim - 2))

    xf = fpad(x).astype(jnp.float32)
    dtf = fpad(dt).astype(jnp.float32)
    bf = fpad(b_mat).astype(jnp.float32).reshape(bsz, nc, CHUNK, SSM_GROUPS, D_STATE)
    cf = fpad(c_mat).astype(jnp.float32).reshape(bsz, nc, CHUNK, SSM_GROUPS, D_STATE)
    xdt = (xf * dtf[..., None]).reshape(bsz, nc, CHUNK, SSM_GROUPS, HEADS_PER_GROUP, SSM_HEAD_DIM)
    a_dt = (dtf * a.astype(jnp.float32)).reshape(bsz, nc, CHUNK, SSM_GROUPS, HEADS_PER_GROUP)
    a_cs = jnp.cumsum(a_dt, axis=2)

    mask = jnp.tril(jnp.ones((CHUNK, CHUNK), dtype=bool))[:, :, None, None]
    diff = a_cs[:, :, :, None] - a_cs[:, :, None, :]
    lmat = jnp.exp(jnp.where(mask, diff, -jnp.inf))
    cb = jnp.einsum("bclgn,bcsgn->bclsg", cf, bf)
    y_diag = jnp.einsum("bclsg,bclsgr,bcsgrp->bclgrp", cb, lmat, xdt)

    decay_states = jnp.exp(a_cs[:, :, -1:] - a_cs)
    states = jnp.einsum("bclgn,bclgr,bclgrp->bcgrpn", bf, decay_states, xdt)
    chunk_decay = jnp.exp(a_cs[:, :, -1])

    def step(h, inp):
        dec, st = inp
        return dec[..., None, None] * h + st, h

    h0 = jnp.zeros((bsz, SSM_GROUPS, HEADS_PER_GROUP, SSM_HEAD_DIM, D_STATE), jnp.float32)
    _, prev = lax.scan(step, h0, (jnp.moveaxis(chunk_decay, 1, 0), jnp.moveaxis(states, 1, 0)))
    prev = jnp.moveaxis(prev, 0, 1)

    y_off = jnp.einsum("bclgn,bcgrpn,bclgr->bclgrp", cf, prev, jnp.exp(a_cs))
    y = (y_diag + y_off).reshape(bsz, lp, SSM_HEADS, SSM_HEAD_DIM)[:, pad:]
    return y.astype(x.dtype)


def mamba2_branch(z, xbc, dt_raw, conv_w, conv_b, dt_bias, a_log, d_skip, norm_w):
    bsz, seqlen, _ = z.shape
    xbc = jax.nn.silu(causal_depthwise_conv(xbc, conv_w, conv_b))
    xs = xbc[..., :D_INNER].reshape(bsz, seqlen, SSM_HEADS, SSM_HEAD_DIM)
    b_mat = xbc[..., D_INNER:D_INNER + SSM_GROUPS * D_STATE].reshape(bsz, seqlen, SSM_GROUPS, D_STATE)
    c_mat = xbc[..., D_INNER + SSM_GROUPS * D_STATE:].reshape(bsz, seqlen, SSM_GROUPS, D_STATE)
    dt = jax.nn.softplus(dt_raw + dt_bias)
    a = -jnp.exp(a_log)
    y = ssd_chunked(xs, dt, a, b_mat, c_mat) + xs * d_skip[:, None]
    y = y.reshape(bsz, seqlen, D_INNER) * jax.nn.silu(z)
    yg = y.reshape(bsz, seqlen, SSM_GROUPS, D_INNER // SSM_GROUPS).astype(jnp.float32)
    yg = yg * lax.rsqrt(jnp.mean(yg * yg, axis=-1, keepdims=True) + EPS)
    return yg.reshape(bsz, seqlen, D_INNER).astype(z.dtype) * norm_w


def setup_inputs(seed: int = 0) -> dict:
    key = jax.random.key(seed)
    ks = jax.random.split(key, 24)
    f32 = jnp.float32
    nrm = lambda k, shape, s: jax.random.normal(k, shape, f32) * s
    dt_init = jnp.exp(jax.random.uniform(ks[9], (DEPTH, SSM_HEADS), f32, np.log(1e-3), np.log(1e-1)))
    return {
        "x": nrm(ks[0], (BATCH, SEQ, D_MODEL), 1.0),
        "meta_tokens": nrm(ks[1], (N_META, D_MODEL), 1.0),
        "mix_norm_w": 1.0 + nrm(ks[2], (DEPTH, D_MODEL), 0.02),
        "w_in": nrm(ks[3], (DEPTH, D_MODEL, IN_COLS), D_MODEL ** -0.5),
        "b_gate": nrm(ks[4], (DEPTH, N_BRANCHES * D_MODEL), 0.02),
        "pool_w_group": nrm(ks[5], (DEPTH, POOL_GROUPS, POOL_GROUP_DIM, POOL_GROUP_DIM), POOL_GROUP_DIM ** -0.5),
        "pool_scale": 1.0 + nrm(ks[6], (DEPTH, POOL_WIDTH), 0.02),
        "w_pool_up": nrm(ks[7], (DEPTH, POOL_WIDTH, D_MODEL), POOL_WIDTH ** -0.5),
        "conv_w": nrm(ks[8], (DEPTH, CONV_WIDTH, D_XBC), CONV_WIDTH ** -0.5),
        "conv_b": nrm(ks[10], (DEPTH, D_XBC), 0.02),
        "dt_bias": dt_init + jnp.log(-jnp.expm1(-dt_init)),
        "a_log": jnp.log(jax.random.uniform(ks[11], (DEPTH, SSM_HEADS), f32, 1.0, 16.0)),
        "d_skip": 1.0 + nrm(ks[12], (DEPTH, SSM_HEADS), 0.02),
        "ssd_norm_w": 1.0 + nrm(ks[13], (DEPTH, D_INNER), 0.02),
        "w_ssd_out": nrm(ks[14], (DEPTH, D_INNER, D_MODEL), D_INNER ** -0.5),
        "w_o": nrm(ks[15], (DEPTH, D_MODEL, D_MODEL), D_MODEL ** -0.5),
        "mlp_norm_w": 1.0 + nrm(ks[16], (DEPTH, D_MODEL), 0.02),
        "w_ff1": nrm(ks[17], (DEPTH, D_MODEL, D_FF), D_MODEL ** -0.5),
        "w_ff2": nrm(ks[18], (DEPTH, D_FF, D_MODEL), 0.5 * D_FF ** -0.5),
        "final_norm_w": 1.0 + nrm(ks[19], (D_MODEL,), 0.02),
    }


def reference(x, meta_tokens, mix_norm_w, w_in, b_gate, pool_w_group, pool_scale, w_pool_up,
              conv_w, conv_b, dt_bias, a_log, d_skip, ssd_norm_w, w_ssd_out, w_o,
              mlp_norm_w, w_ff1, w_ff2, final_norm_w):
    bsz = x.shape[0]
    meta = jnp.broadcast_to(meta_tokens[None].astype(x.dtype), (bsz, N_META, D_MODEL))
    h = jnp.concatenate([meta, x], axis=1)
    for i in range(DEPTH):
        u = rmsnorm(h, mix_norm_w[i])
        proj = u @ w_in[i]
        u_pool = proj[..., OFF_POOL:OFF_Z]
        z = proj[..., OFF_Z:OFF_XBC]
        xbc = proj[..., OFF_XBC:OFF_DT]
        dt_raw = proj[..., OFF_DT:OFF_GATE]
        gates = jax.nn.sigmoid(proj[..., OFF_GATE:] + b_gate[i])
        gate_pool = gates[..., :D_MODEL]
        gate_ssd = gates[..., D_MODEL:]

        y_pool = pool_mixer(u_pool, pool_w_group[i], pool_scale[i]) @ w_pool_up[i]
        y_ssd = mamba2_branch(z, xbc, dt_raw, conv_w[i], conv_b[i], dt_bias[i], a_log[i],
                              d_skip[i], ssd_norm_w[i]) @ w_ssd_out[i]
        h = h + (gate_pool * y_pool + gate_ssd * y_ssd) @ w_o[i]

        v = rmsnorm(h, mlp_norm_w[i])
        hid = jax.nn.relu(v @ w_ff1[i])
        h = h + (hid * hid) @ w_ff2[i]
    out = rmsnorm(h, final_norm_w)
    return out[:, N_META:]
```

```python
import numpy as np
from contextlib import ExitStack
import concourse.bass as bass
import concourse.mybir as mybir
from concourse.bass_utils import run_bass_kernel_spmd

F32 = mybir.dt.float32
BF16 = mybir.dt.bfloat16
ALU = mybir.AluOpType
AF = mybir.ActivationFunctionType
AX = mybir.AxisListType

D = 1024
L = 4
NCH = 17
T = NCH * 128
TILES = [(0, 128)] + [(128 + 512 * i, 128 + 512 * (i + 1)) for i in range(4)]
IN_COLS = 9248
OFF_Z, OFF_XBC, OFF_DT, OFF_GATE = 1024, 3072, 7168, 7200
EPS = 1e-5
PPW = 216


class Reg:
    __slots__ = ("name", "lw", "rd")

    def __init__(self, name=""):
        self.name = name
        self.lw = None
        self.rd = {}


class DSem:
    def __init__(self, sem):
        self.sem = sem
        self.count = 0
        self.last = None


class Op:
    __slots__ = ("eng", "fn", "args", "kw", "raw", "oth", "is_dma", "dsem")


class Prog:
    def __init__(self, nc, stack):
        self.nc = nc
        self.stack = stack
        self.ops = []
        self.last = {}
        self.live_dma = []
        self.nsem = 0

    def newsem(self):
        self.nsem += 1
        return self.stack.enter_context(self.nc.semaphore("s%d" % self.nsem))

    def dsem(self):
        return DSem(self.newsem())

    def op(self, eng, fn, *args, R=(), W=(), dsem=None, deps=(), **kw):
        i = len(self.ops)
        o = Op()
        o.eng, o.fn, o.args, o.kw = eng, fn, args, kw
        o.is_dma = dsem is not None
        o.dsem = dsem
        raw = set()
        oth = set(deps)
        for r in R:
            if r.lw is not None:
                raw.add(r.lw)
        for w in W:
            if w.lw is not None:
                oth.add(w.lw)
            oth.update(w.rd.values())
        if dsem is not None:
            if dsem.last is not None:
                oth.add(dsem.last)
            dsem.last = i
        o.raw, o.oth = raw, oth
        self.ops.append(o)
        wset = set(id(w) for w in W)
        for w in W:
            w.lw = i
            w.rd = {}
        for r in R:
            if id(r) not in wset:
                key = ("d", i) if o.is_dma else eng
                r.rd[key] = i
        if o.is_dma:
            self.live_dma.append(i)
        else:
            self.last[eng] = i
        return i

    def barrier(self):
        deps = set(self.last.values()) | set(self.live_dma)
        self.live_dma = []
        for e in ("pe", "act", "dve", "pool", "sp"):
            self.op(e, None, deps=deps)

    def finalize(self):
        nc = self.nc
        engobj = {"pe": nc.tensor, "act": nc.scalar, "dve": nc.vector, "pool": nc.gpsimd, "sp": nc.sync}
        n = len(self.ops)
        ret = [None] * n
        signal = set()
        for i, o in enumerate(self.ops):
            keep = []
            for j in o.raw | o.oth:
                oj = self.ops[j]
                if oj.fn is None:
                    continue
                if (not o.is_dma) and (not oj.is_dma) and oj.eng == o.eng and o.fn is not None:
                    if o.eng == "pe":
                        continue
                keep.append(j)
            ret[i] = keep
            signal.update(keep)
        sems, cnt, val = {}, {}, {}
        known = {e: {} for e in engobj}
        nwait = 0
        for i, o in enumerate(self.ops):
            E = engobj[o.eng]
            kn = known[o.eng]
            need = {}
            for j in ret[i]:
                s, v = val[j]
                if kn.get(id(s), 0) < v and need.get(id(s), (None, 0))[1] < v:
                    need[id(s)] = (s, v)
            for s, v in need.values():
                E.wait_ge(s, v)
                kn[id(s)] = v
                nwait += 1
            if o.fn is None:
                continue
            r = o.fn(*o.args, **o.kw)
            if o.is_dma:
                insts = r if isinstance(r, list) else [r]
                for ins in insts:
                    ins.then_inc(o.dsem.sem, 16)
                o.dsem.count += len(insts)
                val[i] = (o.dsem.sem, 16 * o.dsem.count)
            elif i in signal:
                if o.eng not in sems or cnt[o.eng] >= 30000:
                    sems[o.eng] = self.newsem()
                    cnt[o.eng] = 0
                cnt[o.eng] += 1
                r.then_inc(sems[o.eng], 1)
                val[i] = (sems[o.eng], cnt[o.eng])
        self.stats = dict(n_ops=n, n_wait=nwait, n_signal=len(signal), nsem=self.nsem)


class Arena:
    def __init__(self, t, nbytes):
        self.t = t
        self.cap = nbytes
        self.off = 0

    def mark(self):
        return self.off

    def release(self, m):
        self.off = m

    def alloc(self, shape, dtype):
        n = int(np.prod(shape))
        esz = 4 if dtype == F32 else 2
        nb = (n * esz + 31) // 32 * 32
        assert self.off + nb <= self.cap, ("SBUF arena overflow", self.off, nb, self.cap)
        a = self.t[:, self.off // 4:(self.off + nb) // 4]
        self.off += nb
        if dtype != F32:
            a = a.bitcast(dtype)
        a = a[:, 0:n]
        if len(shape) == 2:
            a = a.rearrange("p (a b) -> p a b", a=shape[0])
        elif len(shape) == 3:
            a = a.rearrange("p (a b c) -> p a b c", a=shape[0], b=shape[1])
        return a


def build_program(n_layers=L, n_pass=2, dbg=None):
    nc = bass.Bass("TRN2", target_bir_lowering=False)
    dbg = dbg or {}
    stop = dbg.get("stop", "")
    dumps = {}
    with ExitStack() as st:
        P = Prog(nc, st)
        def din(name, shape):
            return nc.dram_tensor(name, list(shape), F32, kind="ExternalInput").ap()

        xT_d = din("xT", [n_pass, D, T])
        outT_d = nc.dram_tensor("outT", [n_pass, D, T], F32, kind="ExternalOutput").ap()
        w_in_d = din("w_in", [L, D, IN_COLS])
        pool_wg_d = din("pool_wg", [L, 4, 256, 256])
        w_pool_up_d = din("w_pool_up", [L, D, D])
        w_ssd_out_d = din("w_ssd_out", [L, 2048, D])
        w_o_d = din("w_o", [L, D, D])
        w_ff1_d = din("w_ff1", [L, D, 4096])
        w_ff2_d = din("w_ff2", [L, 4096, D])
        pp_d = din("pp", [128, L * PPW + 8])
        rb_d = din("rb", [128, L * 96])
        consts_d = din("consts", [128, 5 * 128])
        mask_d = din("maskrow", [n_pass, 128, 128 + 1])
        invc_d = din("invc", [n_pass, 128, 8 * 16])
        bnd0_d = din("bnd0", [L, 128, 4096])
        bnd_s = [nc.dram_tensor("bnd_s%d" % p, [L, 128, 4096], F32).ap() for p in range(n_pass)]
        yn_s = nc.dram_tensor("yn_s", [16, 128, T], BF16).ap()
        gyp_s = nc.dram_tensor("gyp_s", [8, 128, T], BF16).ap()

        cap = 212000 // 32 * 32
        arena_t = st.enter_context(nc.sbuf_tensor("arena", [128, cap // 4], F32))
        A = Arena(arena_t, cap)
        banks = [st.enter_context(nc.psum_tensor("pb%d" % i, [128, 512], F32)) for i in range(8)]
        bankr = [Reg("bank%d" % i) for i in range(8)]

        hT = A.alloc([8, T], F32)
        uT = A.alloc([8, T], BF16)
        hR = [[Reg() for _ in TILES] for _ in range(8)]
        uR = [[Reg() for _ in TILES] for _ in range(8)]
        cst32 = A.alloc([5, 128], F32)
        cstb = A.alloc([3, 128], BF16)
        cR = Reg("consts")
        pp = A.alloc([L * PPW + 8], F32)
        rb = A.alloc([L * 96], F32)
        arow = A.alloc([L * 32], F32)
        maskt = A.alloc([129], F32)
        invc = A.alloc([8, 16], F32)
        ppR, mR = Reg(), Reg()
        epsc = A.alloc([1], F32)
        base_mark = A.mark()

        ident32 = cst32[:, 0, :]
        ones32 = cst32[:, 1, :]
        tril32 = cst32[:, 2, :]
        mgt32 = cst32[:, 3, :]
        identb = cstb[:, 0, :]
        onesb = cstb[:, 1, :]
        mgtb = cstb[:, 2, :]

        def tile_of(col):
            for ti, (a, b) in enumerate(TILES):
                if a <= col < b:
                    return ti
            raise ValueError(col)

        def dump(name, ap, regs):
            shape = list(ap.shape)
            d = nc.dram_tensor("dbg_" + name, shape, ap.dtype, kind="ExternalOutput").ap()
            ds = P.dsem()
            P.op("sp", nc.sync.dma_start, out=d, in_=ap, R=regs, dsem=ds)
            dumps[name] = "dbg_" + name

        ds0 = P.dsem()
        P.op("sp", nc.sync.dma_start, out=cst32.rearrange("p a b -> p (a b)"), in_=consts_d, W=[cR], dsem=ds0)
        ds1 = P.dsem()
        P.op("sp", nc.sync.dma_start, out=pp, in_=pp_d, W=[ppR], dsem=ds1)
        ds2 = P.dsem()
        P.op("sp", nc.sync.dma_start, out=rb, in_=rb_d, W=[ppR], dsem=ds2)
        P.op("dve", nc.vector.tensor_copy, cstb[:, 0:2, :].rearrange("p a b -> p (a b)"), cst32[:, 0:2, :].rearrange("p a b -> p (a b)"), R=[cR], W=[cR])
        P.op("dve", nc.vector.tensor_copy, cstb[:, 2, :], cst32[:, 3, :], R=[cR], W=[cR])
        P.op("dve", nc.vector.memset, epsc, EPS, W=[cR])
        for l in range(L):
            P.op("act", nc.scalar.activation, arow[:, l * 32:(l + 1) * 32], rb[:, l * 96 + 32:l * 96 + 64], AF.Exp, R=[ppR], W=[ppR])
        P.op("dve", nc.vector.tensor_scalar, arow, arow, -1.0, None, ALU.mult, R=[ppR], W=[ppR])

        def ppc(l, off, k=None):
            c = l * PPW + off + (0 if k is None else k)
            return pp[:, c:c + 1]

        def norm_phase(wcol_fn, masked):
            m = A.mark()
            sq = A.alloc([8, 512], BF16)
            rstd = A.alloc([512], F32)
            ntmp = [A.alloc([512], F32) for _ in range(2)]
            ntR = [Reg(), Reg()]
            sqR, rsR = Reg(), Reg()
            for ti, (a, b) in enumerate(TILES):
                n = b - a
                bk = ti % 2
                P.op("act", nc.scalar.activation, sq[:, :, 0:n], hT[:, :, a:b], AF.Square,
                     R=[hR[k][ti] for k in range(8)], W=[sqR])
                for k in range(8):
                    P.op("pe", nc.tensor.matmul, banks[bk][:, 0:n], onesb, sq[:, k, 0:n], start=(k == 0), stop=(k == 7),
                         R=[sqR, cR], W=[bankr[bk]])
                P.op("act", nc.scalar.activation, rstd[:, 0:n], banks[bk][:, 0:n], AF.Sqrt, bias=epsc, scale=1.0 / D,
                     R=[cR], W=[bankr[bk], rsR])
                P.op("dve", nc.vector.reciprocal, rstd[:, 0:n], rstd[:, 0:n], R=[rsR], W=[rsR])
                if masked and ti == 0:
                    P.op("dve", nc.vector.tensor_tensor, rstd[:, 0:128], rstd[:, 0:128], maskt[:, 0:128], ALU.mult,
                         R=[rsR, mR], W=[rsR])
                for k in range(8):
                    if k % 3 != 2:
                        P.op("dve", nc.vector.scalar_tensor_tensor, uT[:, k, a:b], hT[:, k, a:b], wcol_fn(k), rstd[:, 0:n], ALU.mult,
                             ALU.mult, R=[hR[k][ti], rsR, ppR], W=[uR[k][ti]])
                    else:
                        tb = ntmp[(k // 3) % 2]
                        tR = ntR[(k // 3) % 2]
                        P.op("act", nc.scalar.activation, tb[:, 0:n], hT[:, k, a:b], AF.Copy, scale=wcol_fn(k), R=[hR[k][ti], ppR],
                             W=[tR])
                        P.op("pool", nc.gpsimd.tensor_tensor, uT[:, k, a:b], tb[:, 0:n], rstd[:, 0:n], ALU.mult, R=[tR, rsR],
                             W=[uR[k][ti]])
            P.barrier()
            A.release(m)

        w_in_v = [w_in_d[l].rearrange("(k p) c -> p k c", p=128) for l in range(L)]
        DS = {n: P.dsem() for n in ("wdt", "wg", "wgz", "bndi", "bndst", "bndo", "bndoh", "yn", "wpin", "wgp", "wpg",
                                    "wup", "gyp", "gyp2", "wpin2", "wgp2", "wup2", "wso2", "wgs2", "wo2", "w1a2", "w1b2", "w2a2", "w2b2", "wso", "wgs", "wo", "ynt", "gypt", "w1a", "w1b", "w2a", "w2b", "out")}
        bndR = [[Reg() for _ in range(L)] for _ in range(n_pass)]
        bnd0R = Reg()
        ynsR = [[Reg() for _ in TILES] for _ in range(8)]
        gypsR = [[Reg() for _ in TILES] for _ in range(8)]

        def load_blocks(dst, src, nblk, regs, names):
            C = dst.shape[2]
            w = C // nblk
            for i in range(nblk):
                P.op("pool", nc.gpsimd.dma_start, out=dst[:, :, i * w:(i + 1) * w], in_=src[:, :, i * w:(i + 1) * w],
                     W=[regs[i]], dsem=DS[names[i % len(names)]])

        def mm(out, lhsT, rhs, start, stop, R, W):
            P.op("pe", nc.tensor.matmul, out, lhsT, rhs, start=start, stop=stop, R=R, W=W)

        def tile_chunks(ti):
            a, b = TILES[ti]
            return list(range(a // 128, b // 128))

        def ssd_phase(l, ps, bnd_in, bnd_inR, bnd_out, bnd_outR, bndo, bndoR):
            m = A.mark()
            Hst = A.alloc([8, 256], F32)
            HR = [Reg() for _ in range(8)]
            wdt = A.alloc([8, 32], BF16)
            wdtR = Reg()
            bndi = A.alloc([224], F32)
            biR = Reg()
            dtall = A.alloc([NCH, 32], F32)
            adt = A.alloc([NCH, 32], F32)
            eacs = A.alloc([NCH, 32], F32)
            dst = A.alloc([NCH, 32], F32)
            ecd = A.alloc([NCH, 32], F32)
            dtR = Reg()
            wgz = A.alloc([8, 256], BF16)
            wgx = A.alloc([8, 512], BF16)
            wgzR, wgxR = Reg(), Reg()
            NB3 = 3
            xraw = [A.alloc([3 + 512], F32) for _ in range(NB3)]
            xrR = [Reg() for _ in range(NB3)]
            acc = [A.alloc([512], F32) for _ in range(NB3)]
            acR = [Reg() for _ in range(NB3)]
            xc = A.alloc([4, T], BF16)
            xcR = [[Reg() for _ in TILES] for _ in range(4)]
            sz_all = A.alloc([NCH, 256], BF16)
            szR = [Reg() for _ in range(NCH)]

            def ring(n, shape, dt):
                return [A.alloc(shape, dt) for _ in range(n)], [Reg() for _ in range(n)]

            Rm, RmR = ring(2, [512], BF16)
            Btok, BtR = ring(3, [128], BF16)
            xtok, xtR = ring(3, [256], BF16)
            xdt, xdR = ring(3, [256], BF16)
            Gm, GmR = ring(2, [128], BF16)
            Em, EmR = ring(2, [512], BF16)
            xw, xwR = ring(2, [256], BF16)
            Mt, MtR = ring(2, [512], BF16)
            y1, y1R = ring(2, [256], F32)
            y3, y3R = ring(3, [256], F32)
            ssc, sscR = ring(3, [2], F32)
            yn, ynR = ring(2, [256], BF16)
            hb, hbR = ring(2, [256], BF16)
            junk = A.alloc([256], BF16)
            junkR = Reg()
            ynst = [A.alloc([2, 512], BF16) for _ in range(2)]
            ystR = [Reg(), Reg()]
            diagD = A.alloc([4, 128], BF16)
            dDR = Reg()
            if l == 0 and ps == 0:
                print("SSD arena peak", A.off, "cap", A.cap)

            P.op("pool", nc.gpsimd.dma_start, out=wdt, in_=w_in_v[l][:, :, OFF_DT:OFF_DT + 32], W=[wdtR], dsem=DS["wdt"])
            P.op("sp", nc.sync.dma_start, out=bndi, in_=bnd_in[l][:, 2048:2272], R=[bnd_inR], W=[biR], dsem=DS["bndi"])
            P.op("sp", nc.sync.dma_start, out=Hst.rearrange("p a b -> p (a b)"), in_=bnd_in[l][:, 0:2048], R=[bnd_inR],
                 W=HR, dsem=DS["bndst"])

            def ld_x(g):
                v = w_in_v[l]
                return [
                    nc.gpsimd.dma_start(out=wgx[:, :, 0:256], in_=v[:, :, OFF_XBC + g * 256:OFF_XBC + (g + 1) * 256]),
                    nc.gpsimd.dma_start(out=wgx[:, :, 256:384],
                                        in_=v[:, :, OFF_XBC + 2048 + g * 128:OFF_XBC + 2048 + (g + 1) * 128]),
                    nc.gpsimd.dma_start(out=wgx[:, :, 384:512],
                                        in_=v[:, :, OFF_XBC + 3072 + g * 128:OFF_XBC + 3072 + (g + 1) * 128]),
                ]

            def ld_z(g):
                return nc.gpsimd.dma_start(out=wgz, in_=w_in_v[l][:, :, OFF_Z + g * 256:OFF_Z + (g + 1) * 256])

            P.op("pool", ld_x, 0, W=[wgxR], dsem=DS["wg"])
            P.op("pool", ld_z, 0, W=[wgzR], dsem=DS["wgz"])

            dtb = rb[:, l * 96:l * 96 + 32]
            arl = arow[:, l * 32:(l + 1) * 32]
            for ti in range(5):
                chs = tile_chunks(ti)
                nchk = len(chs)
                bk = ti % 2
                for ci, c in enumerate(chs):
                    for k in range(8):
                        mm(banks[bk][:, ci * 32:(ci + 1) * 32], uT[:, k, c * 128:(c + 1) * 128], wdt[:, k, :], k == 0, k == 7,
                           R=[uR[k][ti], wdtR], W=[bankr[bk]])
                c0 = chs[0]
                P.op("dve", nc.vector.tensor_tensor, dtall[:, c0:c0 + nchk, :],
                     banks[bk][:, 0:nchk * 32].rearrange("p (a b) -> p a b", a=nchk),
                     dtb.unsqueeze(1).to_broadcast([128, nchk, 32]), ALU.add, R=[ppR], W=[bankr[bk], dtR])
            P.op("act", nc.scalar.activation, dtall, dtall, AF.Exp, R=[dtR], W=[dtR])
            P.op("act", nc.scalar.activation, dtall, dtall, AF.Ln, bias=1.0, R=[dtR], W=[dtR])
            P.op("dve", nc.vector.tensor_scalar, dtall[:, 0, :], dtall[:, 0, :], maskt[:, 128:129], None, ALU.mult,
                 R=[dtR, mR], W=[dtR])
            P.op("dve", nc.vector.tensor_tensor, adt, dtall, arl.unsqueeze(1).to_broadcast([128, NCH, 32]), ALU.mult,
                 R=[dtR, ppR], W=[dtR])
            for (c0, nchk, bk) in ((0, 16, 0), (16, 1, 1)):
                for ci in range(nchk):
                    mm(banks[bk][:, ci * 32:(ci + 1) * 32], tril32, adt[:, c0 + ci, :], True, True, R=[cR, dtR], W=[bankr[bk]])
                    mm(banks[bk + 2][:, ci * 32:(ci + 1) * 32], ones32, adt[:, c0 + ci, :], True, True, R=[cR, dtR],
                       W=[bankr[bk + 2]])
                pa = banks[bk][:, 0:nchk * 32].rearrange("p (a b) -> p a b", a=nchk)
                pt = banks[bk + 2][:, 0:nchk * 32].rearrange("p (a b) -> p a b", a=nchk)
                sl = slice(c0, c0 + nchk)
                P.op("dve", nc.vector.tensor_copy, eacs[:, sl, :], pa, W=[bankr[bk], dtR])
                P.op("act", nc.scalar.activation, ecd[:, sl, :], pt, AF.Exp, W=[bankr[bk + 2], dtR])
                P.op("dve", nc.vector.tensor_tensor, dst[:, sl, :], pt, eacs[:, sl, :], ALU.subtract, R=[dtR],
                     W=[bankr[bk + 2], dtR])
                P.op("act", nc.scalar.activation, dst[:, sl, :], dst[:, sl, :], AF.Exp, R=[dtR], W=[dtR])
                P.op("act", nc.scalar.activation, eacs[:, sl, :], eacs[:, sl, :], AF.Exp, R=[dtR], W=[dtR])
            if stop == "dt":
                dump("dtall", dtall, [dtR]); dump("dst", dst, [dtR]); dump("ecd", ecd, [dtR])
                return True

            b7bf = banks[7][:].bitcast(BF16)
            par = [0]
            for g in range(8):
                cidx = [2 * g, 2 * g + 1, 16 + g, 24 + g]
                pend = None
                for j in range(4):
                    ci = cidx[j]
                    wc = lambda tap, ci=ci: ppc(l, 72, ci * 4 + tap)
                    nprev = 0
                    for ti, (a, b) in enumerate(TILES):
                        n = b - a
                        p = par[0] % NB3
                        pprev = (par[0] - 1) % NB3
                        bk = par[0] % 2
                        par[0] += 1
                        for k in range(8):
                            mm(banks[bk][:, 0:n], wgx[:, k, j * 128:(j + 1) * 128], uT[:, k, a:b], k == 0, k == 7,
                               R=[wgxR, uR[k][ti]], W=[bankr[bk]])
                        P.op("act", nc.scalar.copy, xraw[p][:, 3:3 + n], banks[bk][:, 0:n], W=[bankr[bk], xrR[p]])
                        if ti == 0:
                            P.op("pool", nc.gpsimd.memset, xraw[p][:, 0:3], 0.0, W=[xrR[p]])
                            P.op("pool", nc.gpsimd.tensor_tensor, xraw[p][:, 128:131], xraw[p][:, 128:131],
                                 bndi[:, ci * 3:ci * 3 + 3], ALU.add, R=[xrR[p], biR], W=[xrR[p]])
                        else:
                            P.op("pool", nc.gpsimd.tensor_copy, xraw[p][:, 0:3], xraw[pprev][:, nprev:nprev + 3],
                                 R=[xrR[pprev]], W=[xrR[p]])
                        if ti == 4:
                            P.op("pool", nc.gpsimd.tensor_copy, bndo[:, ci * 3:ci * 3 + 3], xraw[p][:, n:n + 3], R=[xrR[p]],
                                 W=[bndoR])
                        P.op("act", nc.scalar.activation, acc[p][:, 0:n], xraw[p][:, 0:n], AF.Identity, bias=ppc(l, 40, ci),
                             scale=wc(0), R=[xrR[p], ppR], W=[acR[p]])
                        if pend is not None:
                            pj, pa_, pb_, pp_, pn_, pti = pend
                            P.op("act", nc.scalar.activation, xc[:, pj, pa_:pb_], acc[pp_][:, 0:pn_], AF.Silu, R=[acR[pp_]],
                                 W=[xcR[pj][pti]])
                        for tap in (1, 2, 3):
                            P.op("dve", nc.vector.scalar_tensor_tensor, acc[p][:, 0:n], xraw[p][:, tap:tap + n], wc(tap),
                                 acc[p][:, 0:n], ALU.mult, ALU.add, R=[xrR[p], acR[p], ppR], W=[acR[p]])
                        pend = (j, a, b, p, n, ti)
                        nprev = n
                pj, pa_, pb_, pp_, pn_, pti = pend
                P.op("act", nc.scalar.activation, xc[:, pj, pa_:pb_], acc[pp_][:, 0:pn_], AF.Silu, R=[acR[pp_]], W=[xcR[pj][pti]])
                if stop == "conv" and g == 0:
                    dump("xc", xc, [r for rr in xcR for r in rr])
                    return True
                for c in range(0, NCH, 2):
                    nb = min(2, NCH - c)
                    bk = 2 + (c // 2) % 2
                    for q in range(nb):
                        cq = c + q
                        tq = tile_of(cq * 128)
                        for k in range(8):
                            mm(banks[bk][:, q * 256:(q + 1) * 256], uT[:, k, cq * 128:(cq + 1) * 128], wgz[:, k, :], k == 0, k == 7,
                               R=[uR[k][tq], wgzR], W=[bankr[bk]])
                    P.op("act", nc.scalar.activation, sz_all[:, c:c + nb, :],
                         banks[bk][:, 0:nb * 256].rearrange("p (a b) -> p a b", a=nb), AF.Silu,
                         W=[bankr[bk]] + [szR[c + q] for q in range(nb)])
                if g + 1 < 8:
                    P.op("pool", ld_x, g + 1, W=[wgxR], dsem=DS["wg"])
                    P.op("pool", ld_z, g + 1, W=[wgzR], dsem=DS["wgz"])
                for r in range(4):
                    dcol = rb[:, l * 96 + 64 + 4 * g + r:l * 96 + 64 + 4 * g + r + 1]
                    P.op("dve", nc.vector.tensor_scalar, diagD[:, r, :], identb, dcol, None, ALU.mult, R=[cR, ppR], W=[dDR])

                P.op("act", nc.scalar.copy, hb[1], Hst[:, g, :], R=[HR[g]], W=[hbR[1]])
                bA = [banks[2], banks[3]]
                bAR = [bankr[2], bankr[3]]
                bD = [banks[0], banks[1]]
                bDR = [bankr[0], bankr[1]]
                bY = [banks[4], banks[5]]
                bYR = [bankr[4], bankr[5]]

                def r4(ap):
                    return ap.rearrange("p (r q) -> p r q", r=4)

                def T0(c):
                    c0, c1 = c * 128, (c + 1) * 128
                    ti = tile_of(c0)
                    p2 = c % 2
                    P.op("pool", nc.gpsimd.tensor_tensor, r4(Rm[p2]), tril32.unsqueeze(1).to_broadcast([128, 4, 128]),
                         adt[:, c, 4 * g:4 * g + 4].unsqueeze(2).to_broadcast([128, 4, 128]), ALU.mult, R=[cR, dtR], W=[RmR[p2]])
                    bf = bA[p2][:].bitcast(BF16)
                    P.op("pe", nc.tensor.transpose, bf[:, 256:384], xc[:, 2, c0:c1], identb, R=[xcR[2][ti], cR], W=[bAR[p2]])
                    P.op("pe", nc.tensor.transpose, bf[:, 384:512], xc[:, 0, c0:c1], identb, R=[xcR[0][ti], cR], W=[bAR[p2]])
                    P.op("pe", nc.tensor.transpose, bf[:, 512:640], xc[:, 1, c0:c1], identb, R=[xcR[1][ti], cR], W=[bAR[p2]])
                    mm(bA[p2][:, 0:128], xc[:, 2, c0:c1], xc[:, 3, c0:c1], True, True, R=[xcR[2][ti], xcR[3][ti]], W=[bAR[p2]])

                def T1(c):
                    p2, p3 = c % 2, c % 3
                    bf = bA[p2][:].bitcast(BF16)
                    mm(bD[p2][:, 0:512], mgtb, Rm[p2], True, True, R=[cR, RmR[p2]], W=[bDR[p2]])
                    P.op("act", nc.scalar.copy, Btok[p3], bf[:, 256:384], W=[bAR[p2], BtR[p3]])
                    P.op("act", nc.scalar.copy, xtok[p3], bf[:, 384:640], W=[bAR[p2], xtR[p3]])
                    P.op("dve", nc.vector.tensor_tensor, Gm[p2], bA[p2][:, 0:128], tril32, ALU.mult, R=[cR], W=[bAR[p2], GmR[p2]])
                    P.op("dve", nc.vector.tensor_tensor, r4(xdt[p3]), r4(xtok[p3]),
                         dtall[:, c, 4 * g:4 * g + 4].unsqueeze(2).to_broadcast([128, 4, 64]), ALU.mult, R=[xtR[p3], dtR],
                         W=[xdR[p3]])
                    P.op("act", nc.scalar.activation, Em[p2], bD[p2][:, 0:512], AF.Exp, W=[bDR[p2], EmR[p2]])

                def T2(c):
                    p2, p3 = c % 2, c % 3
                    P.op("pool", nc.gpsimd.tensor_tensor, r4(xw[p2]), r4(xdt[p3]),
                         dst[:, c, 4 * g:4 * g + 4].unsqueeze(2).to_broadcast([128, 4, 64]), ALU.mult, R=[xdR[p3], dtR],
                         W=[xwR[p2]])
                    P.op("pool", nc.gpsimd.tensor_tensor, r4(Mt[p2]), r4(Em[p2]), Gm[p2].unsqueeze(1).to_broadcast([128, 4, 128]),
                         ALU.mult, R=[EmR[p2], GmR[p2]], W=[MtR[p2]])

                def T3(c):
                    p2, p3 = c % 2, c % 3
                    c0, c1 = c * 128, (c + 1) * 128
                    ti = tile_of(c0)
                    mm(banks[6][:, 0:256], Btok[p3], xw[p2], True, True, R=[BtR[p3], xwR[p2]], W=[bankr[6]])
                    H3 = r4(Hst[:, g, :])
                    P.op("dve", nc.vector.tensor_tensor, H3, H3, ecd[:, c, 4 * g:4 * g + 4].unsqueeze(2).to_broadcast([128, 4, 64]),
                         ALU.mult, R=[HR[g], dtR], W=[HR[g]])
                    P.op("dve", nc.vector.tensor_tensor, Hst[:, g, :], banks[6][:, 0:256], Hst[:, g, :], ALU.add, R=[HR[g]],
                         W=[bankr[6], HR[g]])
                    P.op("act", nc.scalar.copy, hb[p2], Hst[:, g, :], R=[HR[g]], W=[hbR[p2]])
                    for r in range(4):
                        mm(bY[p2][:, r * 64:(r + 1) * 64], Mt[p2][:, r * 128:(r + 1) * 128], xdt[p3][:, r * 64:(r + 1) * 64], True,
                           False, R=[MtR[p2], xdR[p3]], W=[bYR[p2]])
                        mm(bY[p2][:, r * 64:(r + 1) * 64], diagD[:, r, :], xtok[p3][:, r * 64:(r + 1) * 64], False, True,
                           R=[dDR, xtR[p3]], W=[bYR[p2]])
                    mm(bY[p2][:, 256:512], xc[:, 3, c0:c1], hb[1 - p2], True, True, R=[xcR[3][ti], hbR[1 - p2]], W=[bYR[p2]])

                def T4(c):
                    p2, p3 = c % 2, c % 3
                    P.op("dve", nc.vector.tensor_tensor, r4(y1[p2]), r4(bY[p2][:, 256:512]),
                         eacs[:, c, 4 * g:4 * g + 4].unsqueeze(2).to_broadcast([128, 4, 64]), ALU.mult, R=[dtR],
                         W=[bYR[p2], y1R[p2]])
                    P.op("dve", nc.vector.tensor_tensor, y1[p2], bY[p2][:, 0:256], y1[p2], ALU.add, R=[y1R[p2]],
                         W=[bYR[p2], y1R[p2]])
                    P.op("pool", nc.gpsimd.tensor_tensor, y3[p3], y1[p2], sz_all[:, c, :], ALU.mult, R=[y1R[p2], szR[c]],
                         W=[y3R[p3]])
                    P.op("pool", nc.gpsimd.memset, ssc[p3][:, 0:1], 0.0, W=[sscR[p3]])

                def T4b(c):
                    p3 = c % 3
                    P.op("dve", nc.vector.scalar_tensor_tensor, junk, y3[p3], 1.0, y3[p3], ALU.mult, ALU.mult,
                         accum_out=ssc[p3][:, 0:1], R=[y3R[p3], sscR[p3]], W=[sscR[p3], junkR])

                def T5(c):
                    p3 = c % 3
                    P.op("act", nc.scalar.activation, ssc[p3][:, 1:2], ssc[p3][:, 0:1], AF.Ln, bias=epsc, scale=1.0 / 256.0,
                         R=[sscR[p3], cR], W=[sscR[p3]])
                    P.op("act", nc.scalar.activation, ssc[p3][:, 1:2], ssc[p3][:, 1:2], AF.Exp, scale=-0.5, R=[sscR[p3]],
                         W=[sscR[p3]])

                def T5b(c):
                    p2, p3 = c % 2, c % 3
                    P.op("dve", nc.vector.tensor_scalar, yn[p2], y3[p3], ssc[p3][:, 1:2], None, ALU.mult, R=[y3R[p3], sscR[p3]],
                         W=[ynR[p2]])

                def T6(c):
                    p2 = c % 2
                    c0, c1 = c * 128, (c + 1) * 128
                    ti = tile_of(c0)
                    a, b = TILES[ti]
                    cc = (c0 - a) // 128
                    q = ti % 2
                    P.op("pe", nc.tensor.transpose, b7bf[:, 0:128], yn[p2][:, 0:128], identb, R=[ynR[p2], cR], W=[bankr[7]])
                    P.op("pe", nc.tensor.transpose, b7bf[:, 128:256], yn[p2][:, 128:256], identb, R=[ynR[p2], cR], W=[bankr[7]])
                    for j in range(2):
                        P.op("act", nc.scalar.activation, ynst[q][:, j, cc * 128:(cc + 1) * 128], b7bf[:, j * 128:(j + 1) * 128],
                             AF.Copy, scale=ppc(l, 200, 2 * g + j), R=[ppR], W=[bankr[7], ystR[q]])
                    if c1 == b:
                        n = b - a
                        P.op("sp", nc.sync.dma_start, out=yn_s[2 * g:2 * g + 2, :, a:b].rearrange("j p t -> p j t"),
                             in_=ynst[q][:, :, 0:n], R=[ystR[q]], W=[ynsR[g][ti]], dsem=DS["yn"])

                def ok(c):
                    return 0 <= c < NCH

                for t in range(NCH + 8):
                    if ok(t - 3): T3(t - 3)
                    if ok(t - 1): T1(t - 1)
                    if ok(t): T0(t)
                    if ok(t - 2): T2(t - 2)
                    if ok(t - 4): T4(t - 4)
                    if ok(t - 6): T5(t - 6)
                    if ok(t - 5): T4b(t - 5)
                    if ok(t - 6): T5b(t - 6)
                    if ok(t - 7): T6(t - 7)
            P.op("sp", nc.sync.dma_start, out=bnd_out[l][:, 0:2048], in_=Hst.rearrange("p a b -> p (a b)"), R=HR, W=[bnd_outR],
                 dsem=DS["bndo"])
            P.barrier()
            A.release(m)
            return False

        def pool_phase(l, bndi_src, bnd_inR, bndo, bndoR):
            m = A.mark()
            wpin = A.alloc([8, 1024], BF16)
            wgp = A.alloc([8, 1024], BF16)
            wpg = A.alloc([4, 2, 256], BF16)
            wup = A.alloc([8, 1024], BF16)
            wR = [Reg() for _ in range(4)]
            wpinR = [Reg() for _ in range(4)]
            wupR = [Reg() for _ in range(4)]
            wgpR = [Reg() for _ in range(4)]
            pu = [A.alloc([2, 16 + 512], F32) for _ in range(2)]
            puR = [Reg(), Reg()]
            Sa = A.alloc([2, 16 + 512], F32)
            Sb = A.alloc([2, 16 + 512], F32)
            carry = A.alloc([8, 16], F32)
            phalo = A.alloc([8, 16], F32)
            pooled = A.alloc([8, 512], BF16)
            ypg = A.alloc([8, 512], BF16)
            gp = [A.alloc([512], BF16) for _ in range(2)]
            gpR = [Reg(), Reg()]
            gypst = [A.alloc([512], BF16) for _ in range(2)]
            gyR = [Reg(), Reg()]
            SaR, SbR, phR = [Reg() for _ in range(3)]
            caR = [Reg() for _ in range(4)]
            poR = [Reg() for _ in range(4)]
            ygR = [Reg() for _ in range(4)]
            if l == 0:
                print("pool arena peak", A.off)
            v = w_in_v[l]
            load_blocks(wpin, v[:, :, 0:1024], 4, wpinR, ["wpin", "wpin2"])
            P.op("pool", nc.gpsimd.dma_start, out=wpg, in_=pool_wg_d[l].rearrange("g (cc p) d -> p g cc d", p=128), W=[wR[2]],
                 dsem=DS["wpg"])
            load_blocks(wup, w_pool_up_d[l].rearrange("(k p) f -> p k f", p=128), 4, wupR, ["wup", "wup2"])
            load_blocks(wgp, v[:, :, OFF_GATE:OFF_GATE + 1024], 4, wgpR, ["wgp", "wgp2"])
            P.op("sp", nc.sync.dma_start, out=phalo.rearrange("p a b -> p (a b)"), in_=bndi_src[l][:, 2144:2272], R=[bnd_inR],
                 W=[phR], dsem=DS["bndi"])
            units = [(ti, g) for ti in range(5) for g in range(4)]

            def stA(u):
                ti, g = units[u]
                a, b = TILES[ti]
                n = b - a
                q = u % 2
                for j in range(2):
                    kk = 2 * g + j
                    for k in range(8):
                        mm(banks[j][:, 0:n], wpin[:, k, kk * 128:(kk + 1) * 128], uT[:, k, a:b], k == 0, k == 7,
                           R=[wpinR[g], uR[k][ti]], W=[bankr[j]])
                    P.op("act", nc.scalar.copy, pu[q][:, j, 16:16 + n], banks[j][:, 0:n], W=[bankr[j], puR[q]])

            def stB(u):
                ti, g = units[u]
                a, b = TILES[ti]
                n = b - a
                q = u % 2
                win = 2 ** (g + 1)
                if ti == 0:
                    P.op("pool", nc.gpsimd.memset, pu[q][:, :, 0:16], 0.0, W=[puR[q]])
                    P.op("pool", nc.gpsimd.tensor_tensor, pu[q][:, :, 16 + 112:16 + 128], pu[q][:, :, 16 + 112:16 + 128],
                         phalo[:, 2 * g:2 * g + 2, :], ALU.add, R=[puR[q], phR], W=[puR[q]])
                else:
                    P.op("pool", nc.gpsimd.tensor_copy, pu[q][:, :, 0:16], carry[:, 2 * g:2 * g + 2, :], R=[caR[g]], W=[puR[q]])
                P.op("pool", nc.gpsimd.tensor_copy, carry[:, 2 * g:2 * g + 2, :], pu[q][:, :, n:n + 16], R=[puR[q]], W=[caR[g]])
                if ti == 4:
                    P.op("pool", nc.gpsimd.tensor_copy,
                         bndo[:, 96 + 2 * g * 16:96 + (2 * g + 2) * 16].rearrange("p (a b) -> p a b", a=2),
                         pu[q][:, :, n:n + 16], R=[puR[q]], W=[bndoR])
                src, srcR = pu[q], puR[q]
                bufs = [(Sa, SaR), (Sb, SbR)]
                sh = 1
                lo = 0
                for lev in range(g + 1):
                    dstb, dstR = bufs[lev % 2]
                    lo = lo + sh
                    P.op("pool", nc.gpsimd.tensor_tensor, dstb[:, :, lo:16 + n], src[:, :, lo:16 + n],
                         src[:, :, lo - sh:16 + n - sh], ALU.add, R=[srcR], W=[dstR])
                    src, srcR = dstb, dstR
                    sh *= 2
                P.op("dve", nc.vector.scalar_tensor_tensor, pooled[:, 2 * g:2 * g + 2, 0:n], src[:, :, 16:16 + n], 1.0 / win,
                     pu[q][:, :, 16:16 + n], ALU.mult, ALU.subtract, R=[srcR, puR[q]], W=[poR[g]])
                if ti == 0:
                    P.op("dve", nc.vector.tensor_tensor, src[:, :, 16 + 112:16 + 128], src[:, :, 16 + 112:16 + 128],
                         invc[:, 2 * g:2 * g + 2, :], ALU.mult, R=[srcR, mR], W=[srcR])
                    P.op("dve", nc.vector.tensor_tensor, pooled[:, 2 * g:2 * g + 2, 112:128], src[:, :, 16 + 112:16 + 128],
                         pu[q][:, :, 16 + 112:16 + 128], ALU.subtract, R=[srcR, puR[q]], W=[poR[g]])

            def stC(u):
                ti, g = units[u]
                a, b = TILES[ti]
                n = b - a
                for dj in range(2):
                    bk = 2 + dj
                    for cc in range(2):
                        mm(banks[bk][:, 0:n], wpg[:, g, cc, dj * 128:(dj + 1) * 128], pooled[:, 2 * g + cc, 0:n], cc == 0,
                           cc == 1, R=[wR[2], poR[g]], W=[bankr[bk]])
                    P.op("act", nc.scalar.activation, ypg[:, 2 * g + dj, 0:n], banks[bk][:, 0:n], AF.Copy,
                         scale=ppc(l, 32, 2 * g + dj), R=[ppR], W=[bankr[bk], ygR[g]])

            def stD(ti):
                a, b = TILES[ti]
                n = b - a
                for f in range(8):
                    fp = f % 2
                    bu, bg = 4 + 2 * fp, 5 + 2 * fp
                    for k in range(8):
                        mm(banks[bu][:, 0:n], wup[:, k, f * 128:(f + 1) * 128], ypg[:, k, 0:n], k == 0, k == 7,
                           R=[wupR[f // 2], ygR[k // 2]], W=[bankr[bu]])
                    for k in range(8):
                        mm(banks[bg][:, 0:n], wgp[:, k, f * 128:(f + 1) * 128], uT[:, k, a:b], k == 0, k == 7,
                           R=[wgpR[f // 2], uR[k][ti]], W=[bankr[bg]])
                    P.op("act", nc.scalar.activation, gp[fp][:, 0:n], banks[bg][:, 0:n], AF.Sigmoid, bias=ppc(l, 16, f), R=[ppR],
                         W=[bankr[bg], gpR[fp]])
                    P.op("dve", nc.vector.tensor_tensor, gypst[fp][:, 0:n], banks[bu][:, 0:n], gp[fp][:, 0:n], ALU.mult,
                         R=[gpR[fp]], W=[bankr[bu], gyR[fp]])
                    P.op("sp", nc.sync.dma_start, out=gyp_s[f, :, a:b], in_=gypst[fp][:, 0:n], R=[gyR[fp]],
                         W=[gypsR[f][ti]], dsem=DS["gyp" if fp == 0 else "gyp2"])

            NU = len(units)
            for step in range(NU + 2):
                if 0 <= step - 2 < NU:
                    stC(step - 2)
                    if units[step - 2][1] == 3:
                        stD(units[step - 2][0])
                if step < NU:
                    stA(step)
                if 0 <= step - 1 < NU:
                    stB(step - 1)
            P.barrier()
            A.release(m)
            return False

        def merge_phase(l):
            m = A.mark()
            wso = A.alloc([16, 1024], BF16)
            wgs = A.alloc([8, 1024], BF16)
            wo = A.alloc([8, 1024], BF16)
            wR = [Reg() for _ in range(3)]
            ynt = A.alloc([16, 256], BF16)
            gypt = A.alloc([8, 256], BF16)
            gsb = A.alloc([256], F32)
            mixed = A.alloc([8, 256], BF16)
            ytR, gtR, gsR, mxR = [Reg() for _ in range(4)]
            wsoR = [Reg() for _ in range(4)]
            wgsR = [Reg() for _ in range(4)]
            woR = [Reg() for _ in range(4)]
            load_blocks(wso, w_ssd_out_d[l].rearrange("(k p) f -> p k f", p=128), 4, wsoR, ["wso", "wso2"])
            load_blocks(wgs, w_in_v[l][:, :, OFF_GATE + 1024:OFF_GATE + 2048], 4, wgsR, ["wgs", "wgs2"])
            load_blocks(wo, w_o_d[l].rearrange("(k p) f -> p k f", p=128), 4, woR, ["wo", "wo2"])
            for ti, (a0, b0) in enumerate(TILES):
                for a in range(a0, b0, 256):
                    b = min(a + 256, b0)
                    n = b - a
                    P.op("sp", nc.sync.dma_start, out=ynt[:, :, 0:n], in_=yn_s[:, :, a:b].rearrange("j p t -> p j t"),
                         R=[ynsR[g][ti] for g in range(8)], W=[ytR], dsem=DS["ynt"])
                    P.op("sp", nc.sync.dma_start, out=gypt[:, :, 0:n], in_=gyp_s[:, :, a:b].rearrange("j p t -> p j t"),
                         R=[gypsR[f][ti] for f in range(8)], W=[gtR], dsem=DS["gypt"])
                    for f in range(8):
                        bk = f % 2
                        for k in range(16):
                            mm(banks[bk][:, 0:n], wso[:, k, f * 128:(f + 1) * 128], ynt[:, k, 0:n], k == 0, k == 15,
                               R=[wsoR[f // 2], ytR], W=[bankr[bk]])
                        for k in range(8):
                            mm(banks[2 + bk][:, 0:n], wgs[:, k, f * 128:(f + 1) * 128], uT[:, k, a:b], k == 0, k == 7,
                               R=[wgsR[f // 2], uR[k][ti]], W=[bankr[2 + bk]])
                        P.op("act", nc.scalar.activation, gsb[:, 0:n], banks[2 + bk][:, 0:n], AF.Sigmoid, bias=ppc(l, 16, 8 + f),
                             R=[ppR], W=[bankr[2 + bk], gsR])
                        P.op("dve", nc.vector.tensor_tensor, gsb[:, 0:n], banks[bk][:, 0:n], gsb[:, 0:n], ALU.mult, R=[gsR],
                             W=[bankr[bk], gsR])
                        P.op("pool", nc.gpsimd.tensor_tensor, mixed[:, f, 0:n], gsb[:, 0:n], gypt[:, f, 0:n], ALU.add,
                             R=[gsR, gtR], W=[mxR])
                    if stop == "merge1" and a == 128:
                        dump("mixed", mixed, [mxR])
                        return True
                    for d in range(8):
                        bk = 4 + d % 2
                        for k in range(8):
                            mm(banks[bk][:, 0:n], wo[:, k, d * 128:(d + 1) * 128], mixed[:, k, 0:n], k == 0, k == 7,
                               R=[woR[d // 2], mxR], W=[bankr[bk]])
                        P.op("dve", nc.vector.tensor_tensor, hT[:, d, a:b], banks[bk][:, 0:n], hT[:, d, a:b], ALU.add,
                             R=[hR[d][ti]], W=[bankr[bk], hR[d][ti]])
            P.barrier()
            A.release(m)
            return False

        def mlp_phase(l):
            m = A.mark()
            w1 = [A.alloc([8, 1024], BF16) for _ in range(2)]
            w2 = [A.alloc([8, 1024], BF16) for _ in range(2)]
            w1R = [[Reg() for _ in range(4)] for _ in range(2)]
            w2R = [[Reg() for _ in range(4)] for _ in range(2)]
            hid = A.alloc([8, 512], BF16)
            tmp = A.alloc([512], F32)
            hdR, tmR = Reg(), Reg()
            for fb in range(4):
                s = fb % 2
                load_blocks(w1[s], w_ff1_d[l].rearrange("(k p) f -> p k f", p=128)[:, :, fb * 1024:(fb + 1) * 1024], 4, w1R[s],
                            ["w1a", "w1a2"] if s == 0 else ["w1b", "w1b2"])
                load_blocks(w2[s], w_ff2_d[l][fb * 1024:(fb + 1) * 1024, :].rearrange("(k p) f -> p k f", p=128), 4, w2R[s],
                            ["w2a", "w2a2"] if s == 0 else ["w2b", "w2b2"])
                for ti, (a, b) in enumerate(TILES):
                    n = b - a
                    for j in range(8):
                        bk = j % 2
                        for k in range(8):
                            mm(banks[bk][:, 0:n], w1[s][:, k, j * 128:(j + 1) * 128], uT[:, k, a:b], k == 0, k == 7,
                               R=[w1R[s][j // 2], uR[k][ti]], W=[bankr[bk]])
                        P.op("act", nc.scalar.activation, tmp[:, 0:n], banks[bk][:, 0:n], AF.Relu, W=[bankr[bk], tmR])
                        P.op("pool", nc.gpsimd.tensor_tensor, hid[:, j, 0:n], tmp[:, 0:n], tmp[:, 0:n], ALU.mult, R=[tmR],
                             W=[hdR])
                    for d in range(8):
                        bk = 2 + d % 2
                        for j in range(8):
                            mm(banks[bk][:, 0:n], w2[s][:, j, d * 128:(d + 1) * 128], hid[:, j, 0:n], j == 0, j == 7,
                               R=[w2R[s][d // 2], hdR], W=[bankr[bk]])
                        P.op("dve", nc.vector.tensor_tensor, hT[:, d, a:b], banks[bk][:, 0:n], hT[:, d, a:b], ALU.add,
                             R=[hR[d][ti]], W=[bankr[bk], hR[d][ti]])
            P.barrier()
            A.release(m)
            return False

        def final_phase(ps):
            m = A.mark()
            sq = A.alloc([8, 512], BF16)
            rstd = A.alloc([512], F32)
            ost = A.alloc([8, 512], F32)
            sqR, rsR, osR = Reg(), Reg(), Reg()
            for ti, (a, b) in enumerate(TILES):
                n = b - a
                bk = ti % 2
                P.op("act", nc.scalar.activation, sq[:, :, 0:n], hT[:, :, a:b], AF.Square,
                     R=[hR[k][ti] for k in range(8)], W=[sqR])
                for k in range(8):
                    mm(banks[bk][:, 0:n], onesb, sq[:, k, 0:n], k == 0, k == 7, R=[sqR, cR], W=[bankr[bk]])
                P.op("act", nc.scalar.activation, rstd[:, 0:n], banks[bk][:, 0:n], AF.Sqrt, bias=epsc, scale=1.0 / D,
                     R=[cR], W=[bankr[bk], rsR])
                P.op("dve", nc.vector.reciprocal, rstd[:, 0:n], rstd[:, 0:n], R=[rsR], W=[rsR])
                for k in range(8):
                    P.op("dve", nc.vector.scalar_tensor_tensor, ost[:, k, 0:n], hT[:, k, a:b], pp[:, L * PPW + k:L * PPW + k + 1],
                         rstd[:, 0:n], ALU.mult, ALU.mult, R=[hR[k][ti], rsR, ppR], W=[osR])
                P.op("sp", nc.sync.dma_start, out=outT_d[ps].rearrange("(k p) t -> p k t", p=128)[:, :, a:b], in_=ost[:, :, 0:n],
                     R=[osR], dsem=DS["out"])
            P.barrier()
            A.release(m)

        done = False
        for ps in range(n_pass):
            bnd_in = bnd0_d if ps == 0 else bnd_s[ps - 1]
            bnd_out = bnd_s[ps]
            dsx = [P.dsem() for _ in range(2)] if ps == 0 else dsx
            for k in range(8):
                P.op("sp", nc.sync.dma_start, out=hT[:, k, :], in_=xT_d[ps, k * 128:(k + 1) * 128, :],
                     W=[hR[k][ti] for ti in range(5)], dsem=dsx[k % 2])
            P.op("sp", nc.sync.dma_start, out=maskt, in_=mask_d[ps], W=[mR], dsem=dsx[0])
            P.op("sp", nc.sync.dma_start, out=invc.rearrange("p a b -> p (a b)"), in_=invc_d[ps], W=[mR], dsem=dsx[1])
            for l in range(n_layers):
                bnd_inR = bnd0R if ps == 0 else bndR[ps - 1][l]
                norm_phase(lambda k, l=l: ppc(l, 0, k), True)
                if stop == "norm":
                    dump("uT", uT, [uR[k][ti] for k in range(8) for ti in range(5)])
                    done = True
                    break
                mk = A.mark()
                bndo = A.alloc([224], F32)
                bndoR = Reg()
                done = ssd_phase(l, ps, bnd_in, bnd_inR, bnd_out, bndR[ps][l], bndo, bndoR)
                if done:
                    break
                done = pool_phase(l, bnd_in, bnd_inR, bndo, bndoR)
                if done:
                    break
                P.op("sp", nc.sync.dma_start, out=bnd_out[l][:, 2048:2272], in_=bndo, R=[bndoR], W=[bndR[ps][l]], dsem=DS["bndoh"])
                P.barrier()
                A.release(mk)
                done = merge_phase(l)
                if done:
                    break
                if dbg.get("dump_mid") and l == 0 and ps == 0:
                    dump("hmix", hT, [hR[k][ti] for k in range(8) for ti in range(5)])
                if stop == "merge" and l == n_layers - 1:
                    dump("hT", hT, [hR[k][ti] for k in range(8) for ti in range(5)])
                    done = True
                    break
                norm_phase(lambda k, l=l: ppc(l, 8, k), False)
                done = mlp_phase(l)
                if done:
                    break
                if stop == "layer" and l == n_layers - 1:
                    dump("hT", hT, [hR[k][ti] for k in range(8) for ti in range(5)])
                    done = True
                    break
            if done:
                break
            final_phase(ps)

        P.barrier()
        P.finalize()
    return nc, dumps, P.stats


def prep_inputs(inp, n_pass=2, n_cores=8):
    f32 = np.float32
    x = np.asarray(inp["x"], f32)
    meta = np.asarray(inp["meta_tokens"], f32)
    pp = np.zeros((128, L * PPW + 8), f32)

    def cols(v, nk):
        return np.asarray(v, f32).reshape(nk, 128).T

    for l in range(L):
        o = l * PPW
        pp[:, o + 0:o + 8] = cols(inp["mix_norm_w"][l], 8)
        pp[:, o + 8:o + 16] = cols(inp["mlp_norm_w"][l], 8)
        pp[:, o + 16:o + 32] = cols(inp["b_gate"][l], 16)
        pp[:, o + 32:o + 40] = cols(inp["pool_scale"][l], 8)
        pp[:, o + 40:o + 72] = cols(inp["conv_b"][l], 32)
        cw = np.asarray(inp["conv_w"][l], f32)
        pp[:, o + 72:o + 200] = cw.reshape(4, 32, 128).transpose(2, 1, 0).reshape(128, 128)
        pp[:, o + 200:o + 216] = cols(inp["ssd_norm_w"][l], 16)
    pp[:, L * PPW:L * PPW + 8] = cols(inp["final_norm_w"], 8)
    rb = np.zeros((128, L * 96), f32)
    for l in range(L):
        rb[:, l * 96:l * 96 + 32] = np.asarray(inp["dt_bias"][l], f32)[None]
        rb[:, l * 96 + 32:l * 96 + 64] = np.asarray(inp["a_log"][l], f32)[None]
        rb[:, l * 96 + 64:l * 96 + 96] = np.asarray(inp["d_skip"][l], f32)[None]
    i = np.arange(128)
    consts = np.zeros((128, 5 * 128), f32)
    consts[:, 0:128] = np.eye(128, dtype=f32)
    consts[:, 128:256] = 1.0
    consts[:, 256:384] = (i[:, None] <= i[None, :]).astype(f32)
    consts[:, 384:512] = (i[:, None] > i[None, :]).astype(f32)
    wins = (2, 4, 8, 16)
    shared = dict(
        w_in=np.asarray(inp["w_in"], f32), pool_wg=np.asarray(inp["pool_w_group"], f32),
        w_pool_up=np.asarray(inp["w_pool_up"], f32), w_ssd_out=np.asarray(inp["w_ssd_out"], f32),
        w_o=np.asarray(inp["w_o"], f32), w_ff1=np.asarray(inp["w_ff1"], f32), w_ff2=np.asarray(inp["w_ff2"], f32),
        pp=pp, rb=rb, consts=consts, bnd0=np.zeros((L, 128, 4096), f32),
    )
    maps = []
    for c in range(n_cores):
        b = c % 4
        xT = np.zeros((n_pass, D, T), f32)
        mask = np.zeros((n_pass, 128, 129), f32)
        invc = np.zeros((n_pass, 128, 8, 16), f32)
        for ps in range(n_pass):
            half = ps if n_pass == 2 else (c // 4)
            if half == 0:
                xT[ps, :, 112:128] = meta.T
                mask[ps, :, 112:128] = 1.0
                mask[ps, 112:128, 128] = 1.0
            xT[ps, :, 128:] = x[b, half * 2048:(half + 1) * 2048, :].T
            for k in range(8):
                w = wins[k // 2]
                if half == 0:
                    invc[ps, :, k, :] = (1.0 / np.minimum(np.arange(16) + 1, w)).astype(f32)[None]
                else:
                    invc[ps, :, k, :] = f32(1.0 / w)
        m = dict(shared)
        m.update(xT=xT, maskrow=mask, invc=invc.reshape(n_pass, 128, 128))
        maps.append(m)
    return maps


_CACHE = {}


def kernel(**inputs):
    n_pass = 2
    if "nc" not in _CACHE:
        _CACHE["nc"] = build_program(L, n_pass)[0]
    nc = _CACHE["nc"]
    maps = prep_inputs(inputs, n_pass)
    res = run_bass_kernel_spmd(nc, maps, core_ids=list(range(8)))
    out = np.zeros((4, 4096, D), np.float32)
    for b in range(4):
        oT = res.results[b]["outT"]
        for ps in range(n_pass):
            out[b, ps * 2048:(ps + 1) * 2048, :] = oT[ps][:, 128:].T
    return out
```
